# Optimizing a Trainium2 kernel written in Bass

```python
import math
import jax, jax.numpy as jnp
from jax import lax
import numpy as np

D_MODEL = 1024
BATCH = 4
SEQ = 4096
DEPTH = 2

GRID_W = 64
CTX_LEN = 256
HEAD_DIM = 64
A_HEADS = 8
A_KV_HEADS = 2
A_GROUP = A_HEADS // A_KV_HEADS
B_HEADS = 8
NA_ROWS = 8
NA_COLS = 16
Q_BLOCK = 128
ROPE_THETA = 10000.0
MLA_HEADS = 16
MLA_Q_LORA = 768
MLA_KV_LORA = 256
MLA_NOPE_DIM = 64
MLA_ROPE_DIM = 32
MLA_V_DIM = 64
D_FF = 2816
CONV_WIDTH = 3
N_MOD = 6
EPS = 1e-6
DEEPNORM_ALPHA = (2 * DEPTH) ** 0.25
DEEPNORM_BETA = (8 * DEPTH) ** -0.25
HEAD_SCALE = HEAD_DIM ** -0.5
MLA_SCALE = (MLA_NOPE_DIM + MLA_ROPE_DIM) ** -0.5
_A_Q = A_HEADS * HEAD_DIM
_A_KV = A_KV_HEADS * HEAD_DIM
_B_W = B_HEADS * HEAD_DIM
EVEN_IN_DIM = _A_Q + 2 * _A_KV + 3 * _B_W
EVEN_SPLITS = (_A_Q, _A_Q + _A_KV, _A_Q + 2 * _A_KV, _A_Q + 2 * _A_KV + _B_W, _A_Q + 2 * _A_KV + 2 * _B_W)
EVEN_OUT_DIM = (A_HEADS + B_HEADS) * HEAD_DIM
ODD_IN_DIM = MLA_Q_LORA + MLA_KV_LORA + MLA_ROPE_DIM
ODD_OUT_DIM = MLA_HEADS * MLA_V_DIM

kernel_name = 'hybrid_gqa_natten_mla_convglu_dit'


def _layer_norm(x, g, b):
    xf = x.astype(jnp.float32)
    mu = jnp.mean(xf, axis=-1, keepdims=True)
    var = jnp.mean(jnp.square(xf - mu), axis=-1, keepdims=True)
    return ((xf - mu) * lax.rsqrt(var + EPS) * g + b).astype(x.dtype)


def _rms_norm(x, g):
    xf = x.astype(jnp.float32)
    return (xf * lax.rsqrt(jnp.mean(jnp.square(xf), axis=-1, keepdims=True) + EPS) * g).astype(x.dtype)


def _axial_rope(length, rot_dim, dtype):
    pos = jnp.arange(length, dtype=jnp.int32)
    rows = (pos // GRID_W).astype(jnp.float32)
    cols = (pos % GRID_W).astype(jnp.float32)
    axis_dim = rot_dim // 2
    inv = ROPE_THETA ** (-jnp.arange(0, axis_dim, 2, dtype=jnp.float32) / axis_dim)
    ang = jnp.concatenate([rows[:, None] * inv, cols[:, None] * inv], axis=-1)
    return jnp.cos(ang).astype(dtype), jnp.sin(ang).astype(dtype)


def _rope(x, cos, sin):
    xp = x.reshape(x.shape[:-1] + (x.shape[-1] // 2, 2))
    x1, x2 = xp[..., 0], xp[..., 1]
    return jnp.stack([x1 * cos - x2 * sin, x1 * sin + x2 * cos], axis=-1).reshape(x.shape)


def _heads(t, n, d):
    b, length, _ = t.shape
    return t.reshape(b, length, n, d).transpose(0, 2, 1, 3)


def _merge_heads(t):
    b, h, length, d = t.shape
    return t.transpose(0, 2, 1, 3).reshape(b, length, h * d)


def _dense_attention(q, k, v, kc, vc, scale):
    b, g, r, length, dk = q.shape
    nb = length // Q_BLOCK
    qb = jnp.moveaxis(q.reshape(b, g, r, nb, Q_BLOCK, dk), 3, 0)

    def one_block(qblk):
        s = jnp.concatenate([jnp.einsum('bgrqd,bgkd->bgrqk', qblk, k),
                             jnp.einsum('bgrqd,bgcd->bgrqc', qblk, kc)], axis=-1)
        p = jax.nn.softmax(s.astype(jnp.float32) * scale, axis=-1).astype(v.dtype)
        return (jnp.einsum('bgrqk,bgkd->bgrqd', p[..., :length], v)
                + jnp.einsum('bgrqc,bgcd->bgrqd', p[..., length:], vc))

    o = lax.map(one_block, qb)
    return jnp.moveaxis(o, 0, 3).reshape(b, g, r, length, -1)


def _context_attention(q, k, v, scale):
    s = jnp.einsum('bgrqd,bgkd->bgrqk', q, k).astype(jnp.float32) * scale
    p = jax.nn.softmax(s, axis=-1).astype(v.dtype)
    return jnp.einsum('bgrqk,bgkd->bgrqd', p, v)


def _neighbourhood_attention(q, k, v, kc, vc, rpb, scale):
    b, h, length, d = q.shape
    rows = length // GRID_W
    wr = min(NA_ROWS, rows)
    qg = q.reshape(b, h, rows, GRID_W, d)
    kg = k.reshape(b, h, rows, GRID_W, d)
    vg = v.reshape(b, h, rows, GRID_W, d)
    col = jnp.arange(GRID_W, dtype=jnp.int32)
    col_start = jnp.clip(col - NA_COLS // 2, 0, GRID_W - NA_COLS)
    col_idx = col_start[:, None] + jnp.arange(NA_COLS, dtype=jnp.int32)[None, :]
    col_bias = rpb[:, :, col_idx - col[:, None] + (NA_COLS - 1)].astype(jnp.float32)
    n_nb = wr * NA_COLS

    def one_row(args):
        r, q_row = args
        rs = jnp.clip(r - wr // 2, 0, rows - wr)
        k_win = lax.dynamic_slice_in_dim(kg, rs, wr, axis=2)[:, :, :, col_idx]
        v_win = lax.dynamic_slice_in_dim(vg, rs, wr, axis=2)[:, :, :, col_idx]
        row_off = rs + jnp.arange(wr, dtype=jnp.int32) - r + (NA_ROWS - 1)
        bias = col_bias[:, row_off].transpose(0, 2, 1, 3)
        s_nb = jnp.einsum('bhqd,bhrqkd->bhqrk', q_row, k_win).astype(jnp.float32) * scale + bias[None]
        s_ctx = jnp.einsum('bhqd,bhcd->bhqc', q_row, kc).astype(jnp.float32) * scale
        p = jax.nn.softmax(jnp.concatenate([s_nb.reshape(b, h, GRID_W, n_nb), s_ctx], axis=-1), axis=-1).astype(v.dtype)
        p_nb = p[..., :n_nb].reshape(b, h, GRID_W, wr, NA_COLS)
        return (jnp.einsum('bhqrk,bhrqkd->bhqd', p_nb, v_win)
                + jnp.einsum('bhqc,bhcd->bhqd', p[..., n_nb:], vc))

    out = lax.map(one_row, (jnp.arange(rows, dtype=jnp.int32), jnp.moveaxis(qg, 2, 0)))
    return jnp.moveaxis(out, 0, 2).reshape(b, h, length, d)


def _even_mixer(h, hc, w_in, q_gain, k_gain, rpb, w_out, need_ctx):
    b, length, _ = h.shape
    cos, sin = _axial_rope(length, HEAD_DIM, h.dtype)

    def project(t):
        qa, ka, va, qb, kb, vb = jnp.split(t @ w_in, EVEN_SPLITS, axis=-1)
        return (_rms_norm(_heads(qa, A_HEADS, HEAD_DIM), q_gain),
                _rms_norm(_heads(ka, A_KV_HEADS, HEAD_DIM), k_gain),
                _heads(va, A_KV_HEADS, HEAD_DIM),
                _heads(qb, B_HEADS, HEAD_DIM), _heads(kb, B_HEADS, HEAD_DIM), _heads(vb, B_HEADS, HEAD_DIM))

    qa, ka, va, qb, kb, vb = project(h)
    qac, kac, vac, qbc, kbc, vbc = project(hc)
    qa = _rope(qa, cos, sin)
    ka = _rope(ka, cos, sin)
    ya = _dense_attention(qa.reshape(b, A_KV_HEADS, A_GROUP, length, HEAD_DIM), ka, va, kac, vac,
                          HEAD_SCALE).reshape(b, A_HEADS, length, HEAD_DIM)
    yb = _neighbourhood_attention(qb, kb, vb, kbc, vbc, rpb, HEAD_SCALE)
    y = _merge_heads(jnp.concatenate([ya, yb], axis=1)) @ w_out
    yc = None
    if need_ctx:
        lc = hc.shape[1]
        yac = _context_attention(qac.reshape(b, A_KV_HEADS, A_GROUP, lc, HEAD_DIM), kac, vac,
                                 HEAD_SCALE).reshape(b, A_HEADS, lc, HEAD_DIM)
        ybc = _context_attention(qbc[:, :, None], kbc, vbc, HEAD_SCALE)[:, :, 0]
        yc = _merge_heads(jnp.concatenate([yac, ybc], axis=1)) @ w_out
    return y, yc


def _odd_mixer(h, hc, w_in, cq_gain, ckv_gain, w_uq, w_ukv, w_out, need_ctx):
    b, length, _ = h.shape
    cos, sin = _axial_rope(length, MLA_ROPE_DIM, h.dtype)

    def compress(t):
        c_q, c_kv, k_r = jnp.split(t @ w_in, (MLA_Q_LORA, MLA_Q_LORA + MLA_KV_LORA), axis=-1)
        return _rms_norm(c_q, cq_gain), _rms_norm(c_kv, ckv_gain), k_r

    def up_q(c_q):
        return jnp.split(_heads(c_q @ w_uq, MLA_HEADS, MLA_NOPE_DIM + MLA_ROPE_DIM), (MLA_NOPE_DIM,), axis=-1)

    def up_kv(c_kv, k_r):
        k_nope, v = jnp.split(_heads(c_kv @ w_ukv, MLA_HEADS, MLA_NOPE_DIM + MLA_V_DIM), (MLA_NOPE_DIM,), axis=-1)
        k_shared = jnp.broadcast_to(k_r[:, None], k_nope.shape[:-1] + (MLA_ROPE_DIM,))
        return jnp.concatenate([k_nope, k_shared], axis=-1), v

    c_q, c_kv, k_r = compress(h)
    cc_q, cc_kv, kc_r = compress(hc)
    q_nope, q_rope = up_q(c_q)
    q = jnp.concatenate([q_nope, _rope(q_rope, cos, sin)], axis=-1)
    k, v = up_kv(c_kv, _rope(k_r, cos, sin))
    kc, vc = up_kv(cc_kv, kc_r)
    y = _merge_heads(_dense_attention(q[:, :, None], k, v, kc, vc, MLA_SCALE)[:, :, 0]) @ w_out
    yc = None
    if need_ctx:
        qc = jnp.concatenate(up_q(cc_q), axis=-1)
        yc = _merge_heads(_context_attention(qc[:, :, None], kc, vc, MLA_SCALE)[:, :, 0]) @ w_out
    return y, yc


def _conv_ffn(h, w_up, conv_w, conv_b, w_down):
    gate, val = jnp.split(h @ w_up, 2, axis=-1)
    gate = lax.conv_general_dilated(gate, conv_w[:, None, :], (1,), ((CONV_WIDTH // 2, CONV_WIDTH // 2),),
                                    dimension_numbers=('NWC', 'WIO', 'NWC'), feature_group_count=D_FF) + conv_b
    return (jax.nn.silu(gate) * val) @ w_down


def _adaln(cond, w, b):
    return jnp.split(jax.nn.silu(cond) @ w + b, N_MOD, axis=-1)


def setup_inputs(seed: int = 0) -> dict:
    key = jax.random.key(seed)
    ks = iter(jax.random.split(key, 40))

    def nrm(shape, scale):
        return jax.random.normal(next(ks), shape, jnp.float32) * scale

    def gain(n):
        return 1.0 + nrm((n,), 0.1)

    d = D_MODEL
    return {
        'x': nrm((BATCH, SEQ, d), 1.0),
        'c': nrm((BATCH, d), 1.0),
        'ctx': nrm((BATCH, CTX_LEN, d), 1.0),
        'c_ctx': nrm((d,), 1.0),
        'l0_w_ada': nrm((d, N_MOD * d), 0.5 * d ** -0.5),
        'l0_b_ada': nrm((N_MOD * d,), 0.1),
        'l0_w_in': nrm((d, EVEN_IN_DIM), d ** -0.5),
        'l0_q_gain': gain(HEAD_DIM),
        'l0_k_gain': gain(HEAD_DIM),
        'l0_rpb': nrm((B_HEADS, 2 * NA_ROWS - 1, 2 * NA_COLS - 1), 0.5),
        'l0_w_out': nrm((EVEN_OUT_DIM, d), DEEPNORM_BETA * EVEN_OUT_DIM ** -0.5),
        'l0_ln1_g': gain(d),
        'l0_ln1_b': nrm((d,), 0.02),
        'l0_w_up': nrm((d, 2 * D_FF), d ** -0.5),
        'l0_conv_w': nrm((CONV_WIDTH, D_FF), CONV_WIDTH ** -0.5),
        'l0_conv_b': nrm((D_FF,), 0.02),
        'l0_w_down': nrm((D_FF, d), DEEPNORM_BETA * D_FF ** -0.5),
        'l0_ln2_g': gain(d),
        'l0_ln2_b': nrm((d,), 0.02),
        'l1_w_ada': nrm((d, N_MOD * d), 0.5 * d ** -0.5),
        'l1_b_ada': nrm((N_MOD * d,), 0.1),
        'l1_w_in': nrm((d, ODD_IN_DIM), d ** -0.5),
        'l1_cq_gain': gain(MLA_Q_LORA),
        'l1_ckv_gain': gain(MLA_KV_LORA),
        'l1_w_uq': nrm((MLA_Q_LORA, MLA_HEADS * (MLA_NOPE_DIM + MLA_ROPE_DIM)), MLA_Q_LORA ** -0.5),
        'l1_w_ukv': nrm((MLA_KV_LORA, MLA_HEADS * (MLA_NOPE_DIM + MLA_V_DIM)), MLA_KV_LORA ** -0.5),
        'l1_w_out': nrm((ODD_OUT_DIM, d), DEEPNORM_BETA * ODD_OUT_DIM ** -0.5),
        'l1_ln1_g': gain(d),
        'l1_ln1_b': nrm((d,), 0.02),
        'l1_w_up': nrm((d, 2 * D_FF), d ** -0.5),
        'l1_conv_w': nrm((CONV_WIDTH, D_FF), CONV_WIDTH ** -0.5),
        'l1_conv_b': nrm((D_FF,), 0.02),
        'l1_w_down': nrm((D_FF, d), DEEPNORM_BETA * D_FF ** -0.5),
        'l1_ln2_g': gain(d),
        'l1_ln2_b': nrm((d,), 0.02),
    }


def reference(x, c, ctx, c_ctx,
              l0_w_ada, l0_b_ada, l0_w_in, l0_q_gain, l0_k_gain, l0_rpb, l0_w_out, l0_ln1_g, l0_ln1_b,
              l0_w_up, l0_conv_w, l0_conv_b, l0_w_down, l0_ln2_g, l0_ln2_b,
              l1_w_ada, l1_b_ada, l1_w_in, l1_cq_gain, l1_ckv_gain, l1_w_uq, l1_w_ukv, l1_w_out, l1_ln1_g, l1_ln1_b,
              l1_w_up, l1_conv_w, l1_conv_b, l1_w_down, l1_ln2_g, l1_ln2_b):
    layers = (
        dict(w_ada=l0_w_ada, b_ada=l0_b_ada, w_in=l0_w_in, q_gain=l0_q_gain, k_gain=l0_k_gain, rpb=l0_rpb,
             w_out=l0_w_out, ln1_g=l0_ln1_g, ln1_b=l0_ln1_b, w_up=l0_w_up, conv_w=l0_conv_w, conv_b=l0_conv_b,
             w_down=l0_w_down, ln2_g=l0_ln2_g, ln2_b=l0_ln2_b),
        dict(w_ada=l1_w_ada, b_ada=l1_b_ada, w_in=l1_w_in, cq_gain=l1_cq_gain, ckv_gain=l1_ckv_gain,
             w_uq=l1_w_uq, w_ukv=l1_w_ukv, w_out=l1_w_out, ln1_g=l1_ln1_g, ln1_b=l1_ln1_b, w_up=l1_w_up,
             conv_w=l1_conv_w, conv_b=l1_conv_b, w_down=l1_w_down, ln2_g=l1_ln2_g, ln2_b=l1_ln2_b),
    )
    xc = ctx
    for i in range(DEPTH):
        p = layers[i]
        need_ctx = i < DEPTH - 1
        sh1, sc1, g1, sh2, sc2, g2 = [m[:, None, :] for m in _adaln(c, p['w_ada'], p['b_ada'])]
        csh1, csc1, cg1, csh2, csc2, cg2 = _adaln(c_ctx, p['w_ada'], p['b_ada'])
        h = x * (1.0 + sc1) + sh1
        hc = xc * (1.0 + csc1) + csh1
        if i % 2 == 0:
            y, yc = _even_mixer(h, hc, p['w_in'], p['q_gain'], p['k_gain'], p['rpb'], p['w_out'], need_ctx)
        else:
            y, yc = _odd_mixer(h, hc, p['w_in'], p['cq_gain'], p['ckv_gain'], p['w_uq'], p['w_ukv'],
                               p['w_out'], need_ctx)
        x = _layer_norm(DEEPNORM_ALPHA * x + g1 * y, p['ln1_g'], p['ln1_b'])
        h = x * (1.0 + sc2) + sh2
        x = _layer_norm(DEEPNORM_ALPHA * x + g2 * _conv_ffn(h, p['w_up'], p['conv_w'], p['conv_b'], p['w_down']),
                        p['ln2_g'], p['ln2_b'])
        if need_ctx:
            xc = _layer_norm(DEEPNORM_ALPHA * xc + cg1 * yc, p['ln1_g'], p['ln1_b'])
            hc = xc * (1.0 + csc2) + csh2
            xc = _layer_norm(DEEPNORM_ALPHA * xc + cg2 * _conv_ffn(hc, p['w_up'], p['conv_w'], p['conv_b'], p['w_down']),
                             p['ln2_g'], p['ln2_b'])
    return x
```

```python
import numpy as np
from contextlib import ExitStack
import concourse.bass as bass
import concourse.mybir as mybir
from concourse.bass_utils import run_bass_kernel_spmd

F32 = mybir.dt.float32
BF16 = mybir.dt.bfloat16
AF = mybir.ActivationFunctionType
ALU = mybir.AluOpType
AX = mybir.AxisListType
ENGS = ['pe', 'act', 'dve', 'pool', 'sp']
EPS = 1e-6
ALPHA = 4.0 ** 0.25
D = 1024
DFF = 2816


class Op:
    __slots__ = ('eng', 'fn', 'edeps', 'ddeps', 'inc', 'cnt', 'dma', 'dsem', 'dcnt', 'idx')


class Prog:
    def __init__(self):
        self.ops = {e: [] for e in ENGS}
        self.last_w = {}
        self.readers = {}
        self.dma_sem_of_key = {}
        self.n_dma_sems = 0
        self.dma_cnt = {}
        self.pending_bar = {e: None for e in ENGS}
        self.n = 0

    def emit(self, eng, fn, reads=(), writes=(), dma=False):
        op = Op()
        op.eng = eng
        op.fn = fn
        op.dma = dma
        op.inc = False
        op.cnt = 0
        op.idx = self.n
        self.n += 1
        edeps = {}
        ddeps = {}
        psr = [k for k in reads if k[:2] == 'ps' and k[2:].isdigit()]
        if psr:
            reads = [k for k in reads if k not in psr]
            writes = list(writes) + psr

        def add(d):
            if d is None or d is op:
                return
            if d.dma:
                if ddeps.get(d.dsem, 0) < d.dcnt:
                    ddeps[d.dsem] = d.dcnt
            else:
                o = edeps.get(d.eng)
                if o is None or o.idx < d.idx:
                    edeps[d.eng] = d

        for k in reads:
            add(self.last_w.get(k))
        for k in writes:
            add(self.last_w.get(k))
            for r in self.readers.get(k, {}).values():
                add(r)
        pb = self.pending_bar[eng]
        if pb is not None:
            for d in pb:
                add(d)
            self.pending_bar[eng] = None
        op.edeps = edeps
        op.ddeps = ddeps
        for k in reads:
            rd = self.readers.setdefault(k, {})
            rd[('d', op.idx) if dma else eng] = op
        for k in writes:
            self.last_w[k] = op
            self.readers[k] = {}
        if dma:
            k0 = writes[0]
            s = self.dma_sem_of_key.get(k0)
            if s is None:
                s = self.n_dma_sems
                self.n_dma_sems += 1
                self.dma_sem_of_key[k0] = s
            c = self.dma_cnt.get(s, 0) + 16
            self.dma_cnt[s] = c
            op.dsem = s
            op.dcnt = c
        self.ops[eng].append(op)
        return op

    def barrier(self):
        deps = []
        for e in ENGS:
            for o in reversed(self.ops[e]):
                if not o.dma:
                    deps.append(o)
                    break
        lastd = {}
        for e in ENGS:
            for o in self.ops[e]:
                if o.dma:
                    lastd[o.dsem] = o
        deps.extend(lastd.values())
        for e in ENGS:
            cur = self.pending_bar[e]
            self.pending_bar[e] = deps if cur is None else (cur + deps)

    def finalize(self, nc, stack):
        for e in ENGS:
            for op in self.ops[e]:
                for de, d in op.edeps.items():
                    if de == 'pe' and e == 'pe':
                        continue
                    d.inc = True
        for e in ENGS:
            c = 0
            for op in self.ops[e]:
                if op.inc:
                    c += 1
                    op.cnt = c
        esem = {e: stack.enter_context(nc.semaphore('s_' + e)) for e in ENGS}
        dsem = [stack.enter_context(nc.semaphore('d%d' % i)) for i in range(self.n_dma_sems)]
        block = stack.enter_context(nc.Block())
        prog = self

        def run(e, eo):
            waited = {}
            for op in prog.ops[e]:
                for de, d in op.edeps.items():
                    if de == 'pe' and e == 'pe':
                        continue
                    key = ('e', de)
                    if waited.get(key, 0) < d.cnt:
                        eo.wait_ge(esem[de], d.cnt)
                        waited[key] = d.cnt
                for s, c in op.ddeps.items():
                    key = ('d', s)
                    if waited.get(key, 0) < c:
                        eo.wait_ge(dsem[s], c)
                        waited[key] = c
                ins = op.fn(eo)
                if op.dma:
                    ins.then_inc(dsem[op.dsem], 16)
                elif op.inc:
                    ins.then_inc(esem[e], 1)
            lastd = {}
            for op in prog.ops[e]:
                if op.dma:
                    lastd[op.dsem] = op.dcnt
            for s, c in lastd.items():
                if waited.get(('d', s), 0) < c:
                    eo.wait_ge(dsem[s], c)

        @block.tensor
        def _(eo):
            run('pe', eo)

        @block.scalar
        def _(eo):
            run('act', eo)

        @block.vector
        def _(eo):
            run('dve', eo)

        @block.gpsimd
        def _(eo):
            run('pool', eo)

        @block.sync
        def _(eo):
            run('sp', eo)


class Arena:
    def __init__(self, nc, stack, nbytes, parent=None, lo=0, hi=None):
        if parent is None:
            self.nbytes = nbytes
            self.t32 = stack.enter_context(nc.sbuf_tensor('arena', [128, nbytes // 4], F32))
            self.t16 = self.t32.bitcast(BF16)
        else:
            self.nbytes = parent.nbytes
            self.t32, self.t16 = parent.t32, parent.t16
        self.lo = lo
        self.hi = self.nbytes if hi is None else hi
        self.top = lo
        self.peak = lo

    def mark(self):
        return self.top

    def release(self, m):
        self.top = m

    def alloc(self, nelem, dt):
        sz = nelem * (4 if dt == F32 else 2)
        sz = (sz + 63) // 64 * 64
        off = self.top
        self.top += sz
        self.peak = max(self.peak, self.top)
        assert self.top <= self.hi, ('SBUF arena overflow', self.top, self.lo, self.hi)
        return Buf(self, off, nelem, dt)


class Buf:
    def __init__(self, arena, off, nelem, dt):
        self.t = arena.t32 if dt == F32 else arena.t16
        self.F = arena.nbytes // (4 if dt == F32 else 2)
        self.base = off // (4 if dt == F32 else 2)
        self.n = nelem
        self.dt = dt

    def ap(self, col0=0, dims=None, p0=0, np_=128):
        if dims is None:
            dims = [(1, self.n - col0)]
        pat = [[self.F, np_]] + [[s, c] for (s, c) in dims]
        return bass.AP(self.t, p0 * self.F + self.base + col0, pat)


class PsumBufs:
    def __init__(self, nc, stack):
        self.t32 = stack.enter_context(nc.psum_tensor('psum', [128, 4096], F32))
        self.t16 = self.t32.bitcast(BF16)

    def ap32(self, col0, dims, p0=0, np_=128):
        pat = [[4096, np_]] + [[s, c] for (s, c) in dims]
        return bass.AP(self.t32, p0 * 4096 + col0, pat)

    def ap16(self, col0, dims, p0=0, np_=128):
        pat = [[8192, np_]] + [[s, c] for (s, c) in dims]
        return bass.AP(self.t16, p0 * 8192 + col0, pat)


ARENA_BYTES = 207 * 1024
A1_BYTES = 131 * 1024


class Builder:
    def __init__(self, layer, dbg=False):
        self.layer = layer
        self.nc = bass.Bass("TRN2", target_bir_lowering=False)
        self.P = Prog()
        self.T = {}
        self.cnt = 0

    def din(self, name, shape):
        self.T[name] = self.nc.dram_tensor(name, list(shape), F32, kind="ExternalInput")

    def dout(self, name, shape):
        self.T[name] = self.nc.dram_tensor(name, list(shape), F32, kind="ExternalOutput")

    def dap(self, name, off, pat):
        return bass.AP(self.T[name], off, [list(p) for p in pat])

    def op(self, eng, name, reads, writes, *a, **k):
        return self.P.emit(eng, lambda e: getattr(e, name)(*a, **k), reads, writes)

    def dma(self, eng, out, in_, reads, writes, **k):
        return self.P.emit(eng, lambda e: e.dma_start(out=out, in_=in_, **k), reads, writes, dma=True)

    def bar(self):
        self.P.barrier()

    def setup_consts(self):
        A = self.A
        self.identf = A.alloc(128, F32)
        self.identb = A.alloc(128, BF16)
        self.onesf = A.alloc(64, F32)
        self.cm = A.alloc(8, F32)
        self.dma('sp', self.identf.ap(), self.dap('ident', 0, [[128, 128], [1, 128]]), [], ['identf'])
        self.op('dve', 'tensor_copy', ['identf'], ['identb'], out=self.identb.ap(), in_=self.identf.ap())
        self.op('pool', 'memset', [], ['onesf'], self.onesf.ap(), 1.0)
        self.op('pool', 'memset', [], ['cm'], self.cm.ap(), -0.5)
        self.lnsc = [dict(st=A.alloc(12, F32), mv=A.alloc(2, F32), rs=A.alloc(1, F32), nm=A.alloc(1, F32))
                     for _ in range(2)]
        self.lncnt = 0

    def adaln(self):
        A, PS = self.A, self.PS
        self.modP = [A.alloc(32, F32) for _ in range(2)]
        self.gb = [[self.A2.alloc(1024, F32), A.alloc(1024, F32)] for _ in range(2)]
        m = A.mark()
        cv = A.alloc(16, F32)
        s = A.alloc(16, F32)
        srep = [A.alloc(1024, BF16) for _ in range(2)]
        modb = [A.alloc(6144, F32) for _ in range(2)]
        was = [A.alloc(4096, BF16) for _ in range(2)]
        bbs = [A.alloc(512, F32) for _ in range(2)]
        tmp = A.alloc(1024, F32)
        self.dma('sp', cv.ap(), self.dap('cvecT', 0, [[16, 128], [1, 16]]), [], ['cv'])
        self.op('act', 'activation', ['cv'], ['s'], out=s.ap(), in_=cv.ap(), func=AF.Silu)
        for v in range(2):
            self.op('dve', 'tensor_copy', ['s'], ['srep%d' % v], out=srep[v].ap(dims=[(128, 8), (1, 128)]),
                    in_=s.ap(v, [(2, 8), (0, 128)]))
        for n in range(12):
            sl = n % 2
            self.dma('pool', was[sl].ap(dims=[(512, 8), (1, 512)]),
                     self.dap('w_ada', n * 512, [[6144, 128], [128 * 6144, 8], [1, 512]]), [], ['wa%d' % sl])
            self.dma('sp', bbs[sl].ap(), self.dap('b_ada', n * 512, [[0, 128], [1, 512]]), [], ['bb%d' % sl])
            for v in range(2):
                bank = v
                for c in range(8):
                    self.op('pe', 'matmul', ['srep%d' % v, 'wa%d' % sl], ['ps%d' % bank],
                            PS.ap32(bank * 512, [(1, 512)]), lhsT=srep[v].ap(c * 128, [(1, 128)]),
                            rhs=was[sl].ap(c * 512, [(1, 512)]), start=(c == 0), stop=(c == 7))
                self.op('dve', 'tensor_tensor', ['ps%d' % bank, 'bb%d' % sl], ['modb%d_%d' % (v, n)],
                        out=modb[v].ap(n * 512, [(1, 512)]), in0=PS.ap32(bank * 512, [(1, 512)]),
                        in1=bbs[sl].ap(), op=ALU.add)
        kinds = [(0, 0.0), (1024, 1.0), (3072, 0.0), (4096, 1.0)]
        for v in range(2):
            for k, (off, addc) in enumerate(kinds):
                rk = ['modb%d_%d' % (v, off // 512), 'modb%d_%d' % (v, off // 512 + 1), 'identf']
                self.op('dve', 'tensor_tensor', rk, ['adtmp'], out=tmp.ap(dims=[(128, 8), (1, 128)]),
                        in0=modb[v].ap(off, [(128, 8), (1, 128)]), in1=self.identf.ap(dims=[(0, 8), (1, 128)]),
                        op=ALU.mult)
                self.op('dve', 'tensor_reduce', ['adtmp'], ['modP%d' % v], out=self.modP[v].ap(k * 8, [(1, 8)]),
                        in_=tmp.ap(dims=[(128, 8), (1, 128)]), axis=AX.X, op=ALU.add)
                if addc:
                    self.op('dve', 'tensor_scalar', ['modP%d' % v], ['modP%d' % v],
                            out=self.modP[v].ap(k * 8, [(1, 8)]), in0=self.modP[v].ap(k * 8, [(1, 8)]),
                            scalar1=1.0, scalar2=None, op0=ALU.add)
            for k, off in enumerate((2048, 5120)):
                rk = ['modb%d_%d' % (v, off // 512), 'modb%d_%d' % (v, off // 512 + 1)]
                self.op('pool', 'tensor_copy', rk, ['gb%d_%d' % (v, k)], out=self.gb[v][k].ap(),
                        in_=modb[v].ap(off, [(1, 1024)]))
        self.bar()
        A.release(m)

    def load_bcast(self, name, off, n, key, A=None):
        b = (A or self.A).alloc(n, F32)
        self.dma('sp', b.ap(), self.dap(name, off, [[0, 128], [1, n]]), [], [key])
        return b

    def alloc_ht_scratch(self):
        A = self.A
        self.xb = [A.alloc(1024, BF16) for _ in range(2)]
        self.htcnt = 0

    def make_hT(self, xkey, xap, v, which, dst_c, dkey_c):
        PS = self.PS
        sl = self.htcnt % 2
        self.htcnt += 1
        bank = 6 + sl
        xb = self.xb[sl]
        self.op('pool', 'tensor_copy', [xkey], ['xb%d' % sl], out=xb.ap(), in_=xap)
        for c in range(8):
            self.op('pe', 'transpose', ['xb%d' % sl, 'identb'], ['ps%d' % bank],
                    out=PS.ap16(bank * 1024 + c * 128, [(1, 128)]), in_=xb.ap(c * 128, [(1, 128)]),
                    identity=self.identb.ap())
        mp = self.modP[v]
        for c in range(8):
            sc = mp.ap((2 * which + 1) * 8 + c, [(1, 1)])
            bi = mp.ap((2 * which) * 8 + c, [(1, 1)])
            src = PS.ap16(bank * 1024 + c * 128, [(1, 128)])
            if True:
                self.op('act', 'activation', ['ps%d' % bank, 'modP%d' % v], [dkey_c(c)], out=dst_c(c), in_=src,
                        func=AF.Identity, scale=sc, bias=bi)
            else:
                self.op('dve', 'tensor_scalar', ['ps%d' % bank, 'modP%d' % v], [dkey_c(c)], out=dst_c(c), in0=src,
                        scalar1=sc, scalar2=bi, op0=ALU.mult, op1=ALU.add)

    def layernorm(self, key, xap, gkey, gbuf, bkey, bbuf):
        sc = self.lnsc[self.lncnt % 2]
        sfx = str(self.lncnt % 2)
        self.lncnt += 1
        st, mv, rs, nm = sc['st'], sc['mv'], sc['rs'], sc['nm']
        self.op('dve', 'bn_stats', [key], ['lst' + sfx], out=st.ap(0, [(1, 6)]), in_=xap(0, 512))
        self.op('dve', 'bn_stats', [key], ['lst2' + sfx], out=st.ap(6, [(1, 6)]), in_=xap(512, 512))
        self.op('dve', 'bn_aggr', ['lst' + sfx, 'lst2' + sfx], ['lmv' + sfx], out=mv.ap(), in_=st.ap())
        self.op('dve', 'tensor_scalar', ['lmv' + sfx], ['lrs' + sfx], out=rs.ap(), in0=mv.ap(1, [(1, 1)]),
                scalar1=EPS, scalar2=None, op0=ALU.add)
        self.op('pool', 'tensor_tensor', ['lrs' + sfx, 'cm'], ['lrs' + sfx], out=rs.ap(), in0=rs.ap(),
                in1=self.cm.ap(0, [(1, 1)]), op=ALU.pow)
        self.op('dve', 'scalar_tensor_tensor', ['lmv' + sfx, 'lrs' + sfx], ['lnm' + sfx], out=nm.ap(),
                in0=mv.ap(0, [(1, 1)]), scalar=-1.0, in1=rs.ap(), op0=ALU.mult, op1=ALU.mult)
        self.op('act', 'activation', [key, 'lrs' + sfx, 'lnm' + sfx], [key], out=xap(0, 1024), in_=xap(0, 1024),
                func=AF.Identity, scale=rs.ap(), bias=nm.ap())
        self.op('dve', 'tensor_tensor', [key, gkey], [key], out=xap(0, 1024), in0=xap(0, 1024), in1=gbuf.ap(),
                op=ALU.mult)
        self.op('dve', 'tensor_tensor', [key, bkey], [key], out=xap(0, 1024), in0=xap(0, 1024), in1=bbuf.ap(),
                op=ALU.add)

    def attention(self, jobs, pt, rec, bcs, otmp):
        PS = self.PS
        steps = []
        for ji, job in enumerate(jobs):
            NQ = job['NQ']
            G = 1024 // NQ
            ents = job['ents']
            ngr = (len(ents) + G - 1) // G
            for g in range(ngr):
                steps.append((ji, g, ents[g * G:(g + 1) * G], g == 0, g == ngr - 1))

        def emit_S(si):
            ji, g, ge, first, last = steps[si]
            job = jobs[ji]
            NQ = job['NQ']
            st = si % 2
            for j, (kk, kap, vk, vap) in enumerate(ge):
                col = st * 1024 + j * NQ
                bank = col // 512
                self.op('pe', 'matmul', list(kk) + list(job['q'][0]), ['ps%d' % bank],
                        PS.ap32(col, [(1, NQ)]), lhsT=kap, rhs=job['q'][1], start=True, stop=True)

        def emit_rest(si):
            ji, g, ge, first, last = steps[si]
            job = jobs[ji]
            NQ = job['NQ']
            st = si % 2
            n = len(ge) * NQ
            banks = ['ps%d' % b for b in range(st * 2, st * 2 + (n + 511) // 512)]
            self.op('act', 'activation', banks, ['pt%d' % st], out=pt[st].ap(0, [(1, n)]),
                    in_=PS.ap32(st * 1024, [(1, n)]), func=AF.Exp, scale=job.get('escale', 1.0))
            if job.get('mask') is not None and g == 0:
                mk, map_, mshape = job['mask']
                self.op('dve', 'tensor_tensor', ['pt%d' % st] + list(mk), ['pt%d' % st],
                        out=pt[st].ap(0, mshape), in0=pt[st].ap(0, mshape), in1=map_, op=ALU.mult)
            ob = 4 + (ji % 2)
            for j, (kk, kap, vk, vap) in enumerate(ge):
                self.op('pe', 'matmul', ['pt%d' % st] + list(vk), ['ps%d' % ob],
                        PS.ap32(ob * 512, [(1, NQ)], np_=65), lhsT=vap, rhs=pt[st].ap(j * NQ, [(1, NQ)]),
                        start=(first and j == 0), stop=(last and j == len(ge) - 1))
            if last:
                rs_ = ji % 2
                self.op('dve', 'reciprocal', ['ps%d' % ob], ['rec%d' % rs_],
                        out=rec[rs_].ap(0, [(1, NQ)], p0=64, np_=1), in_=PS.ap32(ob * 512, [(1, NQ)], p0=64, np_=1))
                self.op('pe', 'matmul', ['rec%d' % rs_, 'onesf'], ['ps6'], PS.ap32(6 * 512, [(1, NQ)], np_=64),
                        lhsT=self.onesf.ap(0, [(1, 64)], p0=64, np_=1), rhs=rec[rs_].ap(0, [(1, NQ)], p0=64, np_=1),
                        start=True, stop=True)
                self.op('act', 'activation', ['ps6'], ['bcs%d' % rs_], out=bcs[rs_].ap(0, [(1, NQ)], np_=64),
                        in_=PS.ap32(6 * 512, [(1, NQ)], np_=64), func=AF.Copy)
                dkey, dap_even, dap_dma = job['dst']
                if dap_even is not None:
                    self.op('dve', 'tensor_tensor', ['ps%d' % ob, 'bcs%d' % rs_], [dkey], out=dap_even,
                            in0=PS.ap32(ob * 512, [(1, NQ)], np_=64), in1=bcs[rs_].ap(0, [(1, NQ)], np_=64),
                            op=ALU.mult)
                else:
                    self.op('dve', 'tensor_tensor', ['ps%d' % ob, 'bcs%d' % rs_], ['otmp%d' % rs_],
                            out=otmp[rs_].ap(0, [(1, NQ)], np_=64), in0=PS.ap32(ob * 512, [(1, NQ)], np_=64),
                            in1=bcs[rs_].ap(0, [(1, NQ)], np_=64), op=ALU.mult)
                    self.dma('sp', dap_dma, otmp[rs_].ap(0, [(1, NQ)], np_=64), ['otmp%d' % rs_], ['attnTd%d' % rs_])

        if not steps:
            return
        emit_S(0)
        for si in range(len(steps)):
            if si + 1 < len(steps):
                emit_S(si + 1)
            emit_rest(si)

    def ffn(self, sets, G=2):
        A, PS = self.A2, self.PS
        m = A.mark()
        convT = A.alloc(88, F32)
        self.dma('sp', convT.ap(), self.dap('convT', 0, [[88, 128], [1, 88]]), [], ['convT'])
        nv = max(s['v'] for s in sets) + 1
        wu = [[A.alloc(8 * G * 128, BF16) for _ in range(2)] for _ in range(2)]
        wdst = A.alloc(G * 1024, F32)
        wd = [A.alloc(G * 1024, BF16) for _ in range(nv)]
        aT = [A.alloc(G * s['ntok'], BF16) for s in sets]
        gbuf = [A.alloc(s['ntok'] + 2, F32) for s in sets]
        cb = [A.alloc(512, F32) for _ in range(2)]
        sb = [A.alloc(512, F32) for _ in range(2)]
        for si_, s in enumerate(sets):
            self.op('pool', 'memset', [], ['gbuf%d' % si_], gbuf[si_].ap(), 0.0)
        ngroups = (22 + G - 1) // G
        cnt = 0
        dcnt = 0
        for gi in range(ngroups):
            j0 = gi * G
            gn = min(G, 22 - j0)
            sl = gi % 2
            for half in range(2):
                self.dma('pool', wu[sl][half].ap(0, [(gn * 128, 8), (1, gn * 128)]),
                         self.dap('w_up', half * DFF + j0 * 128, [[5632, 128], [128 * 5632, 8], [1, gn * 128]]),
                         [], ['wu%d_%d' % (sl, half)])
            self.dma('sp', wdst.ap(0, [(1024, gn), (1, 1024)]),
                     self.dap('w_down', j0 * 128 * 1024, [[1024, 128], [128 * 1024, gn], [1, 1024]]), [], ['wdst'])
            for v in range(nv):
                self.op('pool', 'tensor_tensor', ['wdst', 'gb%d_1' % v], ['wd%d' % v],
                        out=wd[v].ap(0, [(1024, gn), (1, 1024)]), in0=wdst.ap(0, [(1024, gn), (1, 1024)]),
                        in1=self.gb[v][1].ap(0, [(0, gn), (1, 1024)]), op=ALU.mult)
            for jl in range(gn):
                j = j0 + jl
                for si_, s in enumerate(sets):
                    ntok, W, h2T = s['ntok'], s['W'], s['h2T']
                    nblk = (ntok + 511) // 512
                    bw = min(512, ntok)
                    gk = 'gbuf%d' % si_
                    blocks = [(1 + b * bw, bw, 1 + b * bw) for b in range(nblk)]
                    if s['halo']:
                        blocks.append((ntok + 1, 1, ntok + 1))
                    for (hc0, n, gc0) in blocks:
                        bank = cnt % 2
                        cnt += 1
                        for c in range(8):
                            self.op('pe', 'matmul', ['wu%d_0' % sl, s['hkey'] + '_c%d' % c], ['ps%d' % bank],
                                    PS.ap32(bank * 512, [(1, n)]),
                                    lhsT=wu[sl][0].ap(c * gn * 128 + jl * 128, [(1, 128)]),
                                    rhs=h2T.ap(c * W + hc0, [(1, n)]), start=(c == 0), stop=(c == 7))
                        self.op('act', 'activation', ['ps%d' % bank], [gk], out=gbuf[si_].ap(gc0, [(1, n)]),
                                in_=PS.ap32(bank * 512, [(1, n)]), func=AF.Copy)
                    for b in range(nblk):
                        vb = 2 + (cnt % 2)
                        cs = cnt % 2
                        cnt += 1
                        for c in range(8):
                            self.op('pe', 'matmul', ['wu%d_1' % sl, s['hkey'] + '_c%d' % c], ['ps%d' % vb],
                                    PS.ap32(vb * 512, [(1, bw)]),
                                    lhsT=wu[sl][1].ap(c * gn * 128 + jl * 128, [(1, 128)]),
                                    rhs=h2T.ap(c * W + 1 + b * bw, [(1, bw)]), start=(c == 0), stop=(c == 7))
                        ck = 'cb%d' % cs
                        g0 = b * bw
                        self.op('act', 'activation', [gk, 'convT'], [ck], out=cb[cs].ap(0, [(1, bw)]),
                                in_=gbuf[si_].ap(g0 + 1, [(1, bw)]), func=AF.Identity,
                                scale=convT.ap(j * 4 + 1, [(1, 1)]), bias=convT.ap(j * 4 + 3, [(1, 1)]))
                        self.op('dve', 'scalar_tensor_tensor', [gk, 'convT', ck], [ck], out=cb[cs].ap(0, [(1, bw)]),
                                in0=gbuf[si_].ap(g0, [(1, bw)]), scalar=convT.ap(j * 4 + 0, [(1, 1)]),
                                in1=cb[cs].ap(0, [(1, bw)]), op0=ALU.mult, op1=ALU.add)
                        self.op('dve', 'scalar_tensor_tensor', [gk, 'convT', ck], [ck], out=cb[cs].ap(0, [(1, bw)]),
                                in0=gbuf[si_].ap(g0 + 2, [(1, bw)]), scalar=convT.ap(j * 4 + 2, [(1, 1)]),
                                in1=cb[cs].ap(0, [(1, bw)]), op0=ALU.mult, op1=ALU.add)
                        self.op('act', 'activation', [ck], ['sb%d' % cs], out=sb[cs].ap(0, [(1, bw)]),
                                in_=cb[cs].ap(0, [(1, bw)]), func=AF.Silu)
                        self.op('dve', 'tensor_tensor', ['sb%d' % cs, 'ps%d' % vb], ['aT%d' % si_],
                                out=aT[si_].ap(jl * ntok + g0, [(1, bw)]), in0=sb[cs].ap(0, [(1, bw)]),
                                in1=PS.ap32(vb * 512, [(1, bw)]), op=ALU.mult)
            for si_, s in enumerate(sets):
                ntok = s['ntok']
                for ti, (xkey, xap) in enumerate(s['tiles']):
                    bk = 4 + 2 * (dcnt % 2)
                    dcnt += 1
                    for n in range(2):
                        for jl in range(gn):
                            self.op('pe', 'matmul', ['aT%d' % si_, 'wd%d' % s['v']], ['ps%d' % (bk + n)],
                                    PS.ap32((bk + n) * 512, [(1, 512)]),
                                    lhsT=aT[si_].ap(jl * ntok + ti * 128, [(1, 128)]),
                                    rhs=wd[s['v']].ap(jl * 1024 + n * 512, [(1, 512)]),
                                    start=(jl == 0), stop=(jl == gn - 1))
                    pk = ['ps%d' % bk, 'ps%d' % (bk + 1)]
                    if gi == 0:
                        self.op('dve', 'scalar_tensor_tensor', [xkey] + pk, [xkey], out=xap(0, 1024),
                                in0=xap(0, 1024), scalar=ALPHA, in1=PS.ap32(bk * 512, [(1, 1024)]),
                                op0=ALU.mult, op1=ALU.add)
                    else:
                        self.op('dve', 'tensor_tensor', [xkey] + pk, [xkey], out=xap(0, 1024), in0=xap(0, 1024),
                                in1=PS.ap32(bk * 512, [(1, 1024)]), op=ALU.add)
        self.bar()
        A.release(m)
        self.ln2g = self.load_bcast('ln', 2 * 1024, 1024, 'ln2g', A)
        self.ln2b = self.load_bcast('ln', 3 * 1024, 1024, 'ln2b', A)
        for s in sets:
            for ti, (xkey, xap) in enumerate(s['tiles']):
                self.layernorm(xkey, xap, 'ln2g', self.ln2g, 'ln2b', self.ln2b)


NQT = 19
QW = NQT * 128
KW = 34 * 128
KBW = 22 * 128
HEAD_SCALE = 0.125


def build_l0(stop=None):
    import os
    CUT = int(os.environ.get('CUT', '99'))
    B = Builder(0)
    nc = B.nc
    B.din('xa', [34, 128, 1024])
    B.din('cvecT', [128, 16])
    B.din('w_ada', [1024, 6144])
    B.din('b_ada', [6144])
    B.din('w_in', [1024, 2304])
    B.din('qkg', [2, 64])
    B.din('w_out', [1024, 1024])
    B.din('ln', [4, 1024])
    B.din('w_up', [1024, 5632])
    B.din('convT', [128, 88])
    B.din('w_down', [2816, 1024])
    B.din('rope', [32, 128, 64])
    B.din('nbias', [3, 128, 40 * 128])
    B.din('ident', [128, 128])
    B.dout('xo', [16, 128, 1024])
    B.dout('co', [2, 128, 1024])
    with ExitStack() as st:
        A0 = Arena(nc, st, ARENA_BYTES)
        B.A = A = Arena(nc, st, 0, parent=A0, lo=0, hi=A1_BYTES)
        B.A2 = A2 = Arena(nc, st, 0, parent=A0, lo=A1_BYTES, hi=ARENA_BYTES)
        B.PS = PS = PsumBufs(nc, st)
        op, dma, dap = B.op, B.dma, B.dap
        B.setup_consts()
        B.adaln()
        if stop == 'adaln':
            B.P.finalize(nc, st)
            return nc
        gq = B.load_bcast('qkg', 0, 64, 'gq')
        gk = B.load_bcast('qkg', 64, 64, 'gk')
        op('dve', 'tensor_scalar', ['gq'], ['gq'], out=gq.ap(), in0=gq.ap(), scalar1=HEAD_SCALE, scalar2=None,
           op0=ALU.mult)
        ln1g = B.load_bcast('ln', 0, 1024, 'lnb0', A2)
        ln1b = B.load_bcast('ln', 1024, 1024, 'lnb1', A2)
        attnT = A2.alloc(8 * QW, BF16)
        B.alloc_ht_scratch()
        mS = A.mark()
        xt = [A.alloc(1024, F32) for _ in range(2)]
        hT = [A.alloc(1024, BF16) for _ in range(2)]
        ropet = [A.alloc(64, F32) for _ in range(2)]
        sq = A.alloc(512, F32)
        qn = A.alloc(512, F32)
        ss = A.alloc(8, F32)
        rq = A.alloc(8, F32)
        ra, rb, rc, rd = [A.alloc(256, F32) for _ in range(4)]
        qtm = A.alloc(512, BF16)
        ktm = A.alloc(512, BF16)
        mP = A.mark()

        def attn_scratch():
            return ([A.alloc(1024, BF16) for _ in range(2)], [A.alloc(512, F32) for _ in range(2)],
                    [A.alloc(512, F32) for _ in range(2)], [A.alloc(512, BF16) for _ in range(2)])

        def tile_of_q(qi):
            return qi if qi <= 16 else 32 + (qi - 17)

        def load_tile(t, sl):
            dma('sp', xt[sl].ap(), dap('xa', t * 128 * 1024, [[1024, 128], [1, 1024]]), [], ['xt%d' % sl])
            if t < 32:
                dma('sp', ropet[sl].ap(), dap('rope', t * 128 * 64, [[64, 128], [1, 64]]), [], ['ropet%d' % sl])
            v = 0 if t < 32 else 1
            B.make_hT('xt%d' % sl, xt[sl].ap(), v, 0, lambda c: hT[sl].ap(c * 128, [(1, 128)]),
                      lambda c: 'hT%d_%d' % (sl, c))

        hkeys = lambda sl: ['hT%d_%d' % (sl, c) for c in range(8)]

        def rmsnorm_rope(psrc, pkey, nh, gain, gkey, dst, dkey, dstride, t, sl):
            n = nh * 64
            op('act', 'activation', [pkey], ['sq'], out=sq.ap(0, [(1, n)]), in_=PS.ap32(psrc, [(1, n)]),
               func=AF.Square)
            op('dve', 'tensor_reduce', ['sq'], ['ss'], out=ss.ap(0, [(1, nh)]), in_=sq.ap(0, [(64, nh), (1, 64)]),
               axis=AX.X, op=ALU.add)
            op('dve', 'tensor_scalar', ['ss'], ['rq'], out=rq.ap(0, [(1, nh)]), in0=ss.ap(0, [(1, nh)]),
               scalar1=1.0 / 64, scalar2=EPS, op0=ALU.mult, op1=ALU.add)
            op('pool', 'tensor_tensor', ['rq', 'cm'], ['rq'], out=rq.ap(0, [(1, nh)]), in0=rq.ap(0, [(1, nh)]),
               in1=B.cm.ap(0, [(1, nh)]), op=ALU.pow)
            op('dve', 'tensor_tensor', [pkey, 'rq'], ['qn'], out=qn.ap(0, [(64, nh), (1, 64)]),
               in0=PS.ap32(psrc, [(64, nh), (1, 64)]), in1=rq.ap(0, [(1, nh), (0, 64)]), op=ALU.mult)
            op('dve', 'tensor_tensor', ['qn', gkey], ['qn'], out=qn.ap(0, [(64, nh), (1, 64)]),
               in0=qn.ap(0, [(64, nh), (1, 64)]), in1=gain.ap(0, [(0, nh), (1, 64)]), op=ALU.mult)
            if t < 32:
                x1 = qn.ap(0, [(64, nh), (2, 32)])
                x2 = qn.ap(1, [(64, nh), (2, 32)])
                cos = ropet[sl].ap(0, [(0, nh), (1, 32)])
                sin = ropet[sl].ap(32, [(0, nh), (1, 32)])
                rk = ['qn', 'ropet%d' % sl]
                v3 = [(32, nh), (1, 32)]
                op('dve', 'tensor_tensor', rk, ['ra'], out=ra.ap(0, v3), in0=x1, in1=cos, op=ALU.mult)
                op('dve', 'tensor_tensor', rk, ['rb'], out=rb.ap(0, v3), in0=x2, in1=sin, op=ALU.mult)
                op('dve', 'tensor_tensor', ['ra', 'rb'], [dkey + 'e'], out=dst.ap(0, [(dstride, nh), (2, 32)]),
                   in0=ra.ap(0, v3), in1=rb.ap(0, v3), op=ALU.subtract)
                op('pool', 'tensor_tensor', rk, ['rc'], out=rc.ap(0, v3), in0=x1, in1=sin, op=ALU.mult)
                op('pool', 'tensor_tensor', rk, ['rd'], out=rd.ap(0, v3), in0=x2, in1=cos, op=ALU.mult)
                op('pool', 'tensor_tensor', ['rc', 'rd'], [dkey + 'o'], out=dst.ap(1, [(dstride, nh), (2, 32)]),
                   in0=rc.ap(0, v3), in1=rd.ap(0, v3), op=ALU.add)
            else:
                op('dve', 'tensor_copy', ['qn'], [dkey + 'e', dkey + 'o'], out=dst.ap(0, [(dstride, nh), (1, 64)]),
                   in_=qn.ap(0, [(64, nh), (1, 64)]))

        m1 = A.mark()
        QaT = A.alloc(4 * QW, BF16)
        KaT = A.alloc(2 * KW, BF16)
        Va = A.alloc(34 * 2 * 66, BF16)
        m2 = A.mark()
        wg = A.alloc(8 * 768, BF16)
        dma('pool', wg.ap(0, [(768, 8), (1, 512)]), dap('w_in', 0, [[2304, 128], [128 * 2304, 8], [1, 512]]), [], ['wg_q'])
        dma('pool', wg.ap(512, [(768, 8), (1, 256)]), dap('w_in', 512, [[2304, 128], [128 * 2304, 8], [1, 256]]), [], ['wg_kv'])
        op('pool', 'memset', [], ['Va'], Va.ap(), 1.0)
        tiles = list(range(34)) if stop != 'pass1s' else ([0] if CUT > 0 else [])
        for idx, t in enumerate(tiles):
            sl = idx % 2
            load_tile(t, sl)
            if CUT <= 1:
                continue
            has_q = (t <= 16) or (t >= 32)
            qi = t if t <= 16 else (17 + t - 32)
            bq = 2 * sl
            bkv = 2 * sl + 1
            bT = 4 + sl
            if has_q:
                for c in range(8):
                    op('pe', 'matmul', ['hT%d_%d' % (sl, c), 'wg_q'], ['ps%d' % bq], PS.ap32(bq * 512, [(1, 512)]),
                       lhsT=hT[sl].ap(c * 128, [(1, 128)]), rhs=wg.ap(c * 768, [(1, 512)]), start=(c == 0), stop=(c == 7))
            for c in range(8):
                op('pe', 'matmul', ['hT%d_%d' % (sl, c), 'wg_kv'], ['ps%d' % bkv], PS.ap32(bkv * 512, [(1, 256)]),
                   lhsT=hT[sl].ap(c * 128, [(1, 128)]), rhs=wg.ap(c * 768 + 512, [(1, 256)]), start=(c == 0), stop=(c == 7))
            if CUT <= 2:
                continue
            if has_q:
                rmsnorm_rope(bq * 512, 'ps%d' % bq, 8, gq, 'gq', qtm, 'qtm', 64, t, sl)
                if CUT <= 3:
                    continue
                for pr in range(4):
                    op('pe', 'transpose', ['qtme', 'qtmo', 'identb'], ['ps%d' % bT],
                       out=PS.ap16(bT * 1024 + pr * 128, [(1, 128)]), in_=qtm.ap(pr * 128, [(1, 128)]),
                       identity=B.identb.ap())
                op('act', 'activation', ['ps%d' % bT], ['QaT%d' % qi], out=QaT.ap(qi * 128, [(QW, 4), (1, 128)]),
                   in_=PS.ap16(bT * 1024, [(128, 4), (1, 128)]), func=AF.Copy)
            if CUT <= 4:
                continue
            rmsnorm_rope(bkv * 512, 'ps%d' % bkv, 2, gk, 'gk', ktm, 'ktm', 128, t, sl)
            op('pool', 'tensor_copy', ['ktme', 'ktmo'], ['ktmd'], out=ktm.ap(64, [(128, 2), (1, 64)]),
               in_=ktm.ap(0, [(128, 2), (1, 64)]))
            for kvh in range(2):
                op('pe', 'transpose', ['ktme', 'ktmo', 'ktmd', 'identb'], ['ps%d' % bT],
                   out=PS.ap16(bT * 1024 + (4 + kvh) * 128, [(1, 128)]), in_=ktm.ap(kvh * 128, [(1, 128)]),
                   identity=B.identb.ap())
            op('dve', 'tensor_copy', ['ps%d' % bT], ['KaT%d' % t], out=KaT.ap(t * 128, [(KW, 2), (1, 128)]),
               in_=PS.ap16(bT * 1024 + 512, [(128, 2), (1, 128)]))
            op('act', 'activation', ['ps%d' % bkv, 'Va'], ['Va%d' % t], out=Va.ap(t * 132, [(66, 2), (1, 64)]),
               in_=PS.ap32(bkv * 512 + 128, [(64, 2), (1, 64)]), func=AF.Copy)
        if stop in ('pass1', 'pass1s'):
            B.P.finalize(nc, st)
            return nc
        B.bar()
        A.release(m2)
        pt, rec, bcs, otmp = attn_scratch()
        jobs = []
        blocks = [(b * 512, 512, list(range(34))) for b in range(4)] + [(2048, 128, list(range(34))),
                                                                        (17 * 128, 256, [32, 33])]
        for (q0, NQ, kts) in blocks:
            qis = list(range(q0 // 128, (q0 + NQ) // 128))
            for h in range(8):
                kvh, pr, base = h // 4, h // 2, 64 * (h % 2)
                ents = []
                for kt in kts:
                    ents.append((['KaT%d' % kt], KaT.ap(kvh * KW + kt * 128, [(1, 128)], p0=base, np_=64),
                                 ['Va%d' % kt, 'Va'], Va.ap(kt * 132 + kvh * 66, [(1, 65)])))
                dkey = 'attnT_%d_%d' % (pr, q0)
                if base == 0:
                    dst = (dkey + 'a', attnT.ap(pr * QW + q0, [(1, NQ)], np_=64), None)
                else:
                    dst = (dkey + 'b', None, attnT.ap(pr * QW + q0, [(1, NQ)], p0=64, np_=64))
                jobs.append(dict(NQ=NQ, q=(['QaT%d' % qi for qi in qis],
                                           QaT.ap(pr * QW + q0, [(1, NQ)], p0=base, np_=64)),
                                 ents=ents, mask=None, dst=dst))
        if stop == 'gqa1':
            jobs = jobs[:2]
        B.attention(jobs, pt, rec, bcs, otmp)
        if stop in ('gqa', 'gqa1'):
            B.P.finalize(nc, st)
            return nc
        B.bar()
        A.release(m1)
        m1 = A.mark()
        QbT = A.alloc(4 * QW, BF16)
        KbT = A.alloc(4 * KBW, BF16)
        Vb = A.alloc(22 * 8 * 66, BF16)
        m2 = A.mark()
        wn = A.alloc(8 * 1536, BF16)
        for i in range(3):
            dma('pool', wn.ap(i * 512, [(1536, 8), (1, 512)]),
                dap('w_in', 768 + i * 512, [[2304, 128], [128 * 2304, 8], [1, 512]]), [], ['wn%d' % i])
        op('pool', 'memset', [], ['Vb'], Vb.ap(), 1.0)
        qbtm = A.alloc(512, BF16)
        kbtm = A.alloc(512, BF16)
        tiles = list(range(20)) + [32, 33]
        for idx, t in enumerate(tiles):
            sl = idx % 2
            load_tile(t, sl)
            has_q = (t <= 16) or (t >= 32)
            qi = t if t <= 16 else (17 + t - 32)
            ki = t if t < 20 else (20 + t - 32)
            bT = 4 + sl
            for i, need in ((0, has_q), (1, True), (2, True)):
                if not need:
                    continue
                for c in range(8):
                    op('pe', 'matmul', ['hT%d_%d' % (sl, c), 'wn%d' % i], ['ps%d' % i], PS.ap32(i * 512, [(1, 512)]),
                       lhsT=hT[sl].ap(c * 128, [(1, 128)]), rhs=wn.ap(c * 1536 + i * 512, [(1, 512)]),
                       start=(c == 0), stop=(c == 7))
            if has_q:
                op('act', 'activation', ['ps0'], ['qbtm'], out=qbtm.ap(), in_=PS.ap32(0, [(1, 512)]), func=AF.Copy,
                   scale=HEAD_SCALE)
                for pr in range(4):
                    op('pe', 'transpose', ['qbtm', 'identb'], ['ps%d' % bT], out=PS.ap16(bT * 1024 + pr * 128, [(1, 128)]),
                       in_=qbtm.ap(pr * 128, [(1, 128)]), identity=B.identb.ap())
                op('dve', 'tensor_copy', ['ps%d' % bT], ['QbT%d' % qi], out=QbT.ap(qi * 128, [(QW, 4), (1, 128)]),
                   in_=PS.ap16(bT * 1024, [(128, 4), (1, 128)]))
            op('dve', 'tensor_copy', ['ps1'], ['kbtm'], out=kbtm.ap(), in_=PS.ap32(512, [(1, 512)]))
            for pr in range(4):
                op('pe', 'transpose', ['kbtm', 'identb'], ['ps%d' % bT], out=PS.ap16(bT * 1024 + 512 + pr * 128, [(1, 128)]),
                   in_=kbtm.ap(pr * 128, [(1, 128)]), identity=B.identb.ap())
            op('act', 'activation', ['ps%d' % bT], ['KbT%d' % ki], out=KbT.ap(ki * 128, [(KBW, 4), (1, 128)]),
               in_=PS.ap16(bT * 1024 + 512, [(128, 4), (1, 128)]), func=AF.Copy)
            op('act', 'activation', ['ps2', 'Vb'], ['Vb%d' % ki], out=Vb.ap(ki * 528, [(66, 8), (1, 64)]),
               in_=PS.ap32(1024, [(64, 8), (1, 64)]), func=AF.Copy)
        B.bar()
        A.release(m2)
        pt, rec, bcs, otmp = attn_scratch()
        EB = A.alloc(40 * 128, BF16)
        ebst = A.alloc(1024, F32)

        def load_EB(patt):
            for i_ in range(5):
                dma('sp', ebst.ap(), dap('nbias', patt * 128 * 5120 + i_ * 1024, [[5120, 128], [1, 1024]]),
                    [], ['ebst'])
                op('act', 'activation', ['ebst'], ['EB'], out=EB.ap(i_ * 1024, [(1, 1024)]), in_=ebst.ap(),
                   func=AF.Exp)
        segs = [[0], [1], list(range(2, 17)) + [17]]
        for sgi, seg in enumerate(segs):
            load_EB(sgi)
            jobs = []
            for qi in seg:
                if qi <= 16:
                    patt = sgi
                    k0 = max(0, qi - 2)
                    kts = [k0 + s_ for s_ in range(5)] + [20, 21]
                    q0, NQ = qi * 128, 128
                else:
                    kts = [20, 21]
                    q0, NQ = 17 * 128, 256
                    patt = None
                qis = list(range(q0 // 128, (q0 + NQ) // 128))
                for h in range(8):
                    pr, base = h // 2, 64 * (h % 2)
                    ents = []
                    for kt in kts:
                        ents.append((['KbT%d' % kt], KbT.ap(pr * KBW + kt * 128, [(1, 128)], p0=base, np_=64),
                                     ['Vb%d' % kt, 'Vb'], Vb.ap(kt * 528 + h * 66, [(1, 65)])))
                    mask = None
                    if patt is not None:
                        mask = (['EB'], EB.ap(h * 128, [(1024, 5), (1, 128)]), [(128, 5), (1, 128)])
                    dkey = 'attnT_%d_%d' % (4 + pr, q0)
                    if base == 0:
                        dst = (dkey + 'a', attnT.ap((4 + pr) * QW + q0, [(1, NQ)], np_=64), None)
                    else:
                        dst = (dkey + 'b', None, attnT.ap((4 + pr) * QW + q0, [(1, NQ)], p0=64, np_=64))
                    jobs.append(dict(NQ=NQ, q=(['QbT%d' % qi_ for qi_ in qis],
                                               QbT.ap(pr * QW + q0, [(1, NQ)], p0=base, np_=64)),
                                     ents=ents, mask=mask, dst=dst))
            B.attention(jobs, pt, rec, bcs, otmp)
        B.bar()
        A.release(mS)
        xres = A.alloc(16 * 1024, F32)
        xc = A.alloc(2 * 1024, F32)
        HW = 2050
        h2T = A.alloc(8 * HW, BF16)
        h2Tc = A.alloc(8 * 258, BF16)
        op('pool', 'memset', [], ['h2T_c%d' % c for c in range(8)], h2T.ap(), 0.0)
        op('pool', 'memset', [], ['h2Tc_c%d' % c for c in range(8)], h2Tc.ap(), 0.0)
        m1 = A.mark()
        xh = A.alloc(1024, F32)
        h2Th = A.alloc(1024, BF16)
        wo = A2.alloc(8 * 1024, BF16)
        ytmp = A2.alloc(1024, F32)
        dma('pool', wo.ap(0, [(1024, 8), (1, 1024)]), dap('w_out', 0, [[1024, 128], [128 * 1024, 8], [1, 1024]]), [], ['wo'])

        def xres_ap(qi):
            if qi < 16:
                return 'xres%d' % qi, (lambda c0, n, qi=qi: xres.ap(qi * 1024 + c0, [(1, n)]))
            if qi == 16:
                return 'xh', (lambda c0, n: xh.ap(c0, [(1, n)]))
            return 'xc%d' % (qi - 17), (lambda c0, n, qi=qi: xc.ap((qi - 17) * 1024 + c0, [(1, n)]))

        for qi in range(NQT):
            t = tile_of_q(qi)
            v = 0 if qi <= 16 else 1
            xkey, xap = xres_ap(qi)
            dma('sp', xap(0, 1024), dap('xa', t * 128 * 1024, [[1024, 128], [1, 1024]]), [], [xkey])
            bk = 2 * (qi % 2)
            for n in range(2):
                for c in range(8):
                    op('pe', 'matmul', ['wo'], ['ps%d' % (bk + n)],
                       PS.ap32((bk + n) * 512, [(1, 512)]), lhsT=attnT.ap(c * QW + qi * 128, [(1, 128)]),
                       rhs=wo.ap(c * 1024 + n * 512, [(1, 512)]), start=(c == 0), stop=(c == 7))
            op('dve', 'tensor_tensor', ['ps%d' % bk, 'ps%d' % (bk + 1), 'gb%d_0' % v], ['ytmp'], out=ytmp.ap(),
               in0=PS.ap32(bk * 512, [(1, 1024)]), in1=B.gb[v][0].ap(), op=ALU.mult)
            op('dve', 'scalar_tensor_tensor', [xkey, 'ytmp'], [xkey], out=xap(0, 1024),
               in0=xap(0, 1024), scalar=ALPHA, in1=ytmp.ap(), op0=ALU.mult, op1=ALU.add)
            B.layernorm(xkey, xap, 'lnb0', ln1g, 'lnb1', ln1b)
            if qi < 16:
                B.make_hT(xkey, xap(0, 1024), 0, 1, lambda c, qi=qi: h2T.ap(c * HW + 1 + qi * 128, [(1, 128)]),
                          lambda c: 'h2T_c%d' % c)
            elif qi == 16:
                B.make_hT(xkey, xap(0, 1024), 0, 1, lambda c: h2Th.ap(c * 128, [(1, 128)]), lambda c: 'h2Th%d' % c)
                op('dve', 'tensor_copy', ['h2Th%d' % c for c in range(8)], ['h2T_c%d' % c for c in range(8)],
                   out=h2T.ap(2049, [(HW, 8), (1, 1)]), in_=h2Th.ap(0, [(128, 8), (1, 1)]))
            else:
                B.make_hT(xkey, xap(0, 1024), 1, 1, lambda c, qi=qi: h2Tc.ap(c * 258 + 1 + (qi - 17) * 128, [(1, 128)]),
                          lambda c: 'h2Tc_c%d' % c)
        B.bar()
        A.release(m1)
        A2.release(A2.lo)
        sets = [dict(h2T=h2T, hkey='h2T', W=HW, ntok=2048, v=0, halo=True,
                     tiles=[xres_ap(qi) for qi in range(16)]),
                dict(h2T=h2Tc, hkey='h2Tc', W=258, ntok=256, v=1, halo=False,
                     tiles=[xres_ap(qi) for qi in (17, 18)])]
        B.ffn(sets)
        for qi in range(16):
            xkey, xap = xres_ap(qi)
            dma('sp', dap('xo', qi * 128 * 1024, [[1024, 128], [1, 1024]]), xap(0, 1024), [xkey], ['xo%d' % (qi % 4)])
        for i in range(2):
            xkey, xap = xres_ap(17 + i)
            dma('sp', dap('co', i * 128 * 1024, [[1024, 128], [1, 1024]]), xap(0, 1024), [xkey], ['co%d' % i])
        B.P.finalize(nc, st)
    return nc


GRID_W = 64


def _rope_table(tok, rot_dim):
    rows = (tok // GRID_W).astype(np.float32)
    cols = (tok % GRID_W).astype(np.float32)
    axis_dim = rot_dim // 2
    inv = (np.float32(10000.0) ** (-np.arange(0, axis_dim, 2, dtype=np.float32) / np.float32(axis_dim))).astype(np.float32)
    ang = np.concatenate([rows[:, None] * inv, cols[:, None] * inv], axis=-1).astype(np.float32)
    return np.concatenate([np.cos(ang), np.sin(ang)], axis=-1).astype(np.float32)


def _nbias_tables(rpb, seq):
    out = np.full((3, 128, 5, 8, 128), -30000.0, np.float32)
    for patt, qi in enumerate((0, 1, 8)):
        qt = seq[qi * 128:(qi + 1) * 128]
        qr, qc = qt // 64, qt % 64
        rs = np.clip(qr - 4, 0, 56)
        cs = np.clip(qc - 8, 0, 48)
        k0 = max(0, qi - 2)
        for s in range(5):
            kt = seq[(k0 + s) * 128:(k0 + s + 1) * 128]
            kr, kc = kt // 64, kt % 64
            ok = ((kr[:, None] >= rs[None, :]) & (kr[:, None] < rs[None, :] + 8) &
                  (kc[:, None] >= cs[None, :]) & (kc[:, None] < cs[None, :] + 16))
            dr = np.clip(kr[:, None] - qr[None, :] + 7, 0, 14)
            dc = np.clip(kc[:, None] - qc[None, :] + 15, 0, 30)
            for h in range(8):
                vals = rpb[h][dr, dc]
                out[patt, :, s, h, :] = np.where(ok, vals, np.float32(-30000.0))
    return out.reshape(3, 128, 40 * 128)


def _l0_inputs(inputs, b, half):
    seq = np.arange(4096) if half == 0 else np.arange(4095, -1, -1)
    cseq = np.arange(256) if half == 0 else np.arange(255, -1, -1)
    x = inputs['x'][b][seq]
    ctx = inputs['ctx'][b][cseq]
    xa = np.concatenate([x, ctx], axis=0).reshape(34, 128, 1024)
    cvec = np.stack([inputs['c'][b], inputs['c_ctx']], axis=0)
    cvecT = np.ascontiguousarray(cvec.reshape(2, 8, 128).transpose(2, 1, 0)).reshape(128, 16)
    cw = inputs['l0_conv_w']
    if half == 1:
        cw = cw[::-1]
    conv = np.concatenate([cw, inputs['l0_conv_b'][None]], axis=0)
    convT = np.ascontiguousarray(conv.reshape(4, 22, 128).transpose(2, 1, 0)).reshape(128, 88)
    rope = _rope_table(seq, 64).reshape(32, 128, 64)
    return dict(
        xa=np.ascontiguousarray(xa), cvecT=cvecT, w_ada=inputs['l0_w_ada'], b_ada=inputs['l0_b_ada'],
        w_in=inputs['l0_w_in'], qkg=np.stack([inputs['l0_q_gain'], inputs['l0_k_gain']]),
        w_out=inputs['l0_w_out'],
        ln=np.stack([inputs['l0_ln1_g'], inputs['l0_ln1_b'], inputs['l0_ln2_g'], inputs['l0_ln2_b']]),
        w_up=inputs['l0_w_up'], convT=convT, w_down=inputs['l0_w_down'], rope=rope,
        nbias=_nbias_tables(inputs['l0_rpb'], seq), ident=np.eye(128, dtype=np.float32))


def run_l0(inputs, cores=8, stop=None):
    inputs = {k: np.ascontiguousarray(np.asarray(v, dtype=np.float32)) for k, v in inputs.items()}
    nc = build_l0(stop)
    in_maps = [_l0_inputs(inputs, c // 2, c % 2) for c in range(cores)]
    res = run_bass_kernel_spmd(nc, in_maps, core_ids=list(range(cores)))
    x0 = np.zeros((4, 4096, 1024), np.float32)
    c0 = np.zeros((4, 256, 1024), np.float32)
    for c in range(cores):
        b, half = c // 2, c % 2
        xo = res.results[c]['xo'].reshape(2048, 1024)
        co = res.results[c]['co'].reshape(256, 1024)
        if half == 0:
            x0[b, :2048] = xo
            c0[b] = co
        else:
            x0[b, 2048:] = xo[::-1]
    return x0, c0


MLA_SCALE = 96.0 ** -0.5
NQ1 = 17
CQW = NQ1 * 128


def build_l1(stop=None):
    import os
    CUT = int(os.environ.get('CUT', '99'))
    B = Builder(1)
    nc = B.nc
    B.din('xa', [34, 128, 1024])
    B.din('cvecT', [128, 16])
    B.din('w_ada', [1024, 6144])
    B.din('b_ada', [6144])
    B.din('w_in', [1024, 1056])
    B.din('gains', [1024])
    B.din('w_uq', [768, 1536])
    B.din('w_ukv', [256, 2048])
    B.din('w_out', [1024, 1024])
    B.din('ln', [4, 1024])
    B.din('w_up', [1024, 5632])
    B.din('convT', [128, 88])
    B.din('w_down', [2816, 1024])
    B.din('rope', [32, 128, 32])
    B.din('ident', [128, 128])
    B.din('sel', [32, 96])
    B.dout('xo', [16, 128, 1024])
    with ExitStack() as st:
        A0 = Arena(nc, st, ARENA_BYTES)
        B.A = A = Arena(nc, st, 0, parent=A0, lo=0, hi=A1_BYTES)
        B.A2 = A2 = Arena(nc, st, 0, parent=A0, lo=A1_BYTES, hi=ARENA_BYTES)
        B.PS = PS = PsumBufs(nc, st)
        op, dma, dap = B.op, B.dma, B.dap
        B.setup_consts()
        B.adaln()
        ln1g = B.load_bcast('ln', 0, 1024, 'lnb0', A2)
        ln1b = B.load_bcast('ln', 1024, 1024, 'lnb1', A2)
        attnT = A2.alloc(8 * CQW, BF16)
        mA2 = A2.mark()
        B.alloc_ht_scratch()
        gains = B.load_bcast('gains', 0, 1024, 'gains')
        self_ = A.alloc(96, F32)
        selb = A.alloc(96, BF16)
        dma('sp', self_.ap(np_=32), dap('sel', 0, [[96, 32], [1, 96]]), [], ['self'])
        op('dve', 'tensor_copy', ['self'], ['selb'], out=selb.ap(np_=32), in_=self_.ap(np_=32))
        mS = A.mark()
        cqT = A.alloc(6 * CQW, BF16)
        ckvT = A.alloc(2 * KW, BF16)
        krT = A.alloc(KW, BF16)
        m2 = A.mark()
        xt = [A.alloc(1024, F32) for _ in range(2)]
        hT = [A.alloc(1024, BF16) for _ in range(2)]
        ropet = [A.alloc(32, F32) for _ in range(2)]
        wi = A.alloc(8 * 1056, BF16)
        sq = A.alloc(768, F32)
        cn = A.alloc(768, F32)
        cbf = A.alloc(768, BF16)
        kvbf = A.alloc(256, BF16)
        ss = A.alloc(1, F32)
        rq = A.alloc(1, F32)
        krf = A.alloc(32, F32)
        krbf = A.alloc(32, BF16)
        ra, rb, rc, rd = [A.alloc(64, F32) for _ in range(4)]
        dma('pool', wi.ap(0, [(1056, 8), (1, 1056)]), dap('w_in', 0, [[1056, 128], [128 * 1056, 8], [1, 1056]]), [], ['wi'])

        def load_tile(t, sl):
            dma('sp', xt[sl].ap(), dap('xa', t * 128 * 1024, [[1024, 128], [1, 1024]]), [], ['xt%d' % sl])
            if t < 32:
                dma('sp', ropet[sl].ap(), dap('rope', t * 128 * 32, [[32, 128], [1, 32]]), [], ['ropet%d' % sl])
            v = 0 if t < 32 else 1
            B.make_hT('xt%d' % sl, xt[sl].ap(), v, 0, lambda c: hT[sl].ap(c * 128, [(1, 128)]),
                      lambda c: 'hT%d_%d' % (sl, c))

        def rms(pcol, n, pkeys, gain_ap, outbuf, outkey):
            op('act', 'activation', pkeys, ['sq'], out=sq.ap(0, [(1, n)]), in_=PS.ap32(pcol, [(1, n)]), func=AF.Square)
            op('dve', 'tensor_reduce', ['sq'], ['ss'], out=ss.ap(), in_=sq.ap(0, [(1, n)]), axis=AX.X, op=ALU.add)
            op('dve', 'tensor_scalar', ['ss'], ['rq'], out=rq.ap(), in0=ss.ap(), scalar1=1.0 / n, scalar2=EPS,
               op0=ALU.mult, op1=ALU.add)
            op('pool', 'tensor_tensor', ['rq', 'cm'], ['rq'], out=rq.ap(), in0=rq.ap(), in1=B.cm.ap(0, [(1, 1)]),
               op=ALU.pow)
            op('act', 'activation', list(pkeys) + ['rq'], ['cn'], out=cn.ap(0, [(1, n)]), in_=PS.ap32(pcol, [(1, n)]),
               func=AF.Identity, scale=rq.ap())
            op('dve', 'tensor_tensor', ['cn', 'gains'], [outkey], out=outbuf.ap(0, [(1, n)]), in0=cn.ap(0, [(1, n)]),
               in1=gain_ap, op=ALU.mult)

        for idx, t in enumerate(range(34) if CUT > 90 else range(CUT > 0)):
            sl = idx % 2
            load_tile(t, sl)
            if CUT <= 1:
                continue
            has_q = t <= 16
            bT = 4 + sl
            if has_q:
                for n, (c0, w) in enumerate(((0, 512), (512, 256))):
                    for c in range(8):
                        op('pe', 'matmul', ['hT%d_%d' % (sl, c), 'wi'], ['ps%d' % n], PS.ap32(n * 512, [(1, w)]),
                           lhsT=hT[sl].ap(c * 128, [(1, 128)]), rhs=wi.ap(c * 1056 + c0, [(1, w)]),
                           start=(c == 0), stop=(c == 7))
            for c in range(8):
                op('pe', 'matmul', ['hT%d_%d' % (sl, c), 'wi'], ['ps2'], PS.ap32(1024, [(1, 288)]),
                   lhsT=hT[sl].ap(c * 128, [(1, 128)]), rhs=wi.ap(c * 1056 + 768, [(1, 288)]),
                   start=(c == 0), stop=(c == 7))
            if CUT <= 2:
                continue
            if has_q:
                rms(0, 768, ['ps0', 'ps1'], gains.ap(0, [(1, 768)]), cbf, 'cbf')
                for c in range(6):
                    op('pe', 'transpose', ['cbf', 'identb'], ['ps%d' % bT], out=PS.ap16(bT * 1024 + c * 128, [(1, 128)]),
                       in_=cbf.ap(c * 128, [(1, 128)]), identity=B.identb.ap())
                op('act', 'activation', ['ps%d' % bT], ['cqT%d' % t], out=cqT.ap(t * 128, [(CQW, 6), (1, 128)]),
                   in_=PS.ap16(bT * 1024, [(128, 6), (1, 128)]), func=AF.Copy)
            if CUT <= 3:
                continue
            rms(1024, 256, ['ps2'], gains.ap(768, [(1, 256)]), kvbf, 'kvbf')
            if CUT <= 4:
                continue
            op('act', 'activation', ['ps2'], ['krf'], out=krf.ap(), in_=PS.ap32(1024 + 256, [(1, 32)]), func=AF.Copy)
            if t < 32:
                x1 = krf.ap(0, [(2, 16)])
                x2 = krf.ap(1, [(2, 16)])
                cos = ropet[sl].ap(0, [(1, 16)])
                sin = ropet[sl].ap(16, [(1, 16)])
                rk = ['krf', 'ropet%d' % sl]
                v1 = [(1, 16)]
                op('dve', 'tensor_tensor', rk, ['ra'], out=ra.ap(0, v1), in0=x1, in1=cos, op=ALU.mult)
                op('dve', 'tensor_tensor', rk, ['rb'], out=rb.ap(0, v1), in0=x2, in1=sin, op=ALU.mult)
                op('dve', 'tensor_tensor', ['ra', 'rb'], ['krbfe'], out=krbf.ap(0, [(2, 16)]), in0=ra.ap(0, v1),
                   in1=rb.ap(0, v1), op=ALU.subtract)
                op('pool', 'tensor_tensor', rk, ['rc'], out=rc.ap(0, v1), in0=x1, in1=sin, op=ALU.mult)
                op('pool', 'tensor_tensor', rk, ['rd'], out=rd.ap(0, v1), in0=x2, in1=cos, op=ALU.mult)
                op('pool', 'tensor_tensor', ['rc', 'rd'], ['krbfo'], out=krbf.ap(1, [(2, 16)]), in0=rc.ap(0, v1),
                   in1=rd.ap(0, v1), op=ALU.add)
            else:
                op('dve', 'tensor_copy', ['krf'], ['krbfe', 'krbfo'], out=krbf.ap(), in_=krf.ap())
            for c in range(2):
                op('pe', 'transpose', ['kvbf', 'identb'], ['ps3'], out=PS.ap16(3 * 1024 + c * 128, [(1, 128)]),
                   in_=kvbf.ap(c * 128, [(1, 128)]), identity=B.identb.ap())
            if CUT > 5:
              op('pe', 'transpose', ['krbfe', 'krbfo', 'identb'], ['ps3'],
               out=PS.ap16(3 * 1024 + 256, [(1, 128)], np_=32), in_=krbf.ap(0, [(1, 32)]), identity=B.identb.ap())
            op('dve', 'tensor_copy', ['ps3'], ['ckvT%d' % t], out=ckvT.ap(t * 128, [(KW, 2), (1, 128)]),
               in_=PS.ap16(3 * 1024, [(128, 2), (1, 128)]))
            if CUT > 6:
              op('act', 'activation', ['ps3'], ['krT%d' % t], out=krT.ap(t * 128, [(1, 128)], np_=32),
               in_=PS.ap16(3 * 1024 + 256, [(1, 128)], np_=32), func=AF.Copy)
        if stop == 'pass':
            B.P.finalize(nc, st)
            return nc
        B.bar()
        A.release(m2)
        QT = A.alloc(2 * CQW, BF16)
        KT = A.alloc(2 * KW, BF16)
        Vg = A.alloc(34 * 2 * 66, BF16)
        wq = A.alloc(6 * 192, BF16)
        wkv = A.alloc(2 * 256, BF16)
        wpad = A.alloc(2 * 2 * 96, BF16)
        qtm = A.alloc(192, BF16)
        qf = A.alloc(64, F32)
        ra, rb, rc, rd = [A.alloc(32, F32) for _ in range(4)]
        ropeq = A.alloc(17 * 32, F32)
        pt = [A2.alloc(1024, BF16) for _ in range(2)]
        rec = [A2.alloc(512, F32) for _ in range(2)]
        bcs = [A2.alloc(512, F32) for _ in range(2)]
        otmp = [A2.alloc(512, BF16) for _ in range(2)]
        dma('sp', ropeq.ap(0, [(32, 17), (1, 32)]), dap('rope', 0, [[32, 128], [128 * 32, 17], [1, 32]]), [], ['ropeq'])
        op('pool', 'memset', [], ['Vg'], Vg.ap(), 1.0)
        kcnt = 0
        for g in range(8):
            dma('pool', wq.ap(0, [(192, 6), (1, 192)]), dap('w_uq', g * 192, [[1536, 128], [128 * 1536, 6], [1, 192]]),
                [], ['wq'])
            dma('pool', wkv.ap(0, [(256, 2), (1, 256)]), dap('w_ukv', g * 256, [[2048, 128], [128 * 2048, 2], [1, 256]]),
                [], ['wkv'])
            op('pool', 'memset', [], ['wpad'], wpad.ap(), 0.0)
            op('pool', 'tensor_copy', ['wkv', 'wpad'], ['wpad'], out=wpad.ap(0, [(192, 2), (96, 2), (1, 64)]),
               in_=wkv.ap(0, [(256, 2), (128, 2), (1, 64)]))
            for qi in range(17):
                bk = qi % 2
                bT = 4 + qi % 2
                for c in range(6):
                    op('pe', 'matmul', ['cqT%d' % qi, 'wq'], ['ps%d' % bk], PS.ap32(bk * 512, [(1, 192)]),
                       lhsT=cqT.ap(c * CQW + qi * 128, [(1, 128)]), rhs=wq.ap(c * 192, [(1, 192)]),
                       start=(c == 0), stop=(c == 5))
                op('act', 'activation', ['ps%d' % bk], ['qtmn'], out=qtm.ap(0, [(96, 2), (1, 64)]),
                   in_=PS.ap32(bk * 512, [(96, 2), (1, 64)]), func=AF.Copy)
                op('act', 'activation', ['ps%d' % bk], ['qf'], out=qf.ap(0, [(32, 2), (1, 32)]),
                   in_=PS.ap32(bk * 512 + 64, [(96, 2), (1, 32)]), func=AF.Copy)
                x1 = qf.ap(0, [(32, 2), (2, 16)])
                x2 = qf.ap(1, [(32, 2), (2, 16)])
                cos = ropeq.ap(qi * 32, [(0, 2), (1, 16)])
                sin = ropeq.ap(qi * 32 + 16, [(0, 2), (1, 16)])
                v2 = [(16, 2), (1, 16)]
                rk = ['qf', 'ropeq']
                op('dve', 'tensor_tensor', rk, ['ra'], out=ra.ap(0, v2), in0=x1, in1=cos, op=ALU.mult)
                op('dve', 'tensor_tensor', rk, ['rb'], out=rb.ap(0, v2), in0=x2, in1=sin, op=ALU.mult)
                op('dve', 'tensor_tensor', ['ra', 'rb'], ['qtme'], out=qtm.ap(64, [(96, 2), (2, 16)]), in0=ra.ap(0, v2),
                   in1=rb.ap(0, v2), op=ALU.subtract)
                op('pool', 'tensor_tensor', rk, ['rc'], out=rc.ap(0, v2), in0=x1, in1=sin, op=ALU.mult)
                op('pool', 'tensor_tensor', rk, ['rd'], out=rd.ap(0, v2), in0=x2, in1=cos, op=ALU.mult)
                op('pool', 'tensor_tensor', ['rc', 'rd'], ['qtmo'], out=qtm.ap(65, [(96, 2), (2, 16)]), in0=rc.ap(0, v2),
                   in1=rd.ap(0, v2), op=ALU.add)
                for hh in range(2):
                    op('pe', 'transpose', ['qtmn', 'qtme', 'qtmo', 'identb'], ['ps%d' % bT],
                       out=PS.ap16(bT * 1024 + hh * 128, [(1, 128)], np_=96), in_=qtm.ap(hh * 96, [(1, 96)]),
                       identity=B.identb.ap())
                op('dve', 'tensor_copy', ['ps%d' % bT], ['QT%d' % qi], out=QT.ap(qi * 128, [(CQW, 2), (1, 128)], np_=96),
                   in_=PS.ap16(bT * 1024, [(128, 2), (1, 128)], np_=96))
            for hh in range(2):
                for kb in range(9):
                    n = 512 if kb < 8 else 256
                    col = kb * 512
                    bk = 2 + (kcnt % 2)
                    kcnt += 1
                    rkeys = ['ckvT%d' % t_ for t_ in range(col // 128, (col + n) // 128)]
                    op('pe', 'matmul', rkeys + ['wpad'], ['ps%d' % bk], PS.ap32(bk * 512, [(1, n)], np_=96),
                       lhsT=wpad.ap(hh * 96, [(1, 96)]), rhs=ckvT.ap(col, [(1, n)]), start=True, stop=False)
                    op('pe', 'matmul', rkeys + ['wpad'], ['ps%d' % bk], PS.ap32(bk * 512, [(1, n)], np_=96),
                       lhsT=wpad.ap(192 + hh * 96, [(1, 96)]), rhs=ckvT.ap(KW + col, [(1, n)]), start=False, stop=False)
                    op('pe', 'matmul', ['krT%d' % t_ for t_ in range(col // 128, (col + n) // 128)] + ['selb'],
                       ['ps%d' % bk], PS.ap32(bk * 512, [(1, n)], np_=96), lhsT=selb.ap(0, [(1, 96)], np_=32),
                       rhs=krT.ap(col, [(1, n)], np_=32), start=False, stop=True)
                    if kcnt % 2 == 0:
                        op('act', 'activation', ['ps%d' % bk], ['KT%d_%d' % (hh, kb)],
                           out=KT.ap(hh * KW + col, [(1, n)], np_=96), in_=PS.ap32(bk * 512, [(1, n)], np_=96), func=AF.Copy)
                    else:
                        op('dve', 'tensor_copy', ['ps%d' % bk], ['KT%d_%d' % (hh, kb)],
                           out=KT.ap(hh * KW + col, [(1, n)], np_=96), in_=PS.ap32(bk * 512, [(1, n)], np_=96))
            for kt in range(34):
                for hh in range(2):
                    for c in range(2):
                        op('pe', 'matmul', ['ckvT%d' % kt, 'wkv'], ['ps7'], PS.ap32(7 * 512 + hh * 64, [(1, 64)]),
                           lhsT=ckvT.ap(c * KW + kt * 128, [(1, 128)]), rhs=wkv.ap(c * 256 + hh * 128 + 64, [(1, 64)]),
                           start=(c == 0), stop=(c == 1))
                op('act', 'activation', ['ps7', 'Vg'], ['Vg%d' % kt], out=Vg.ap(kt * 132, [(66, 2), (1, 64)]),
                   in_=PS.ap32(7 * 512, [(64, 2), (1, 64)]), func=AF.Copy)
            if stop == 'g0build':
                B.P.finalize(nc, st)
                return nc
            jobs = []
            for (q0, NQ) in [(b * 512, 512) for b in range(4)] + [(2048, 128)]:
                qis = list(range(q0 // 128, (q0 + NQ) // 128))
                for hh in range(2):
                    ents = [(['KT%d_%d' % (hh, kt // 4)], KT.ap(hh * KW + kt * 128, [(1, 128)], np_=96),
                             ['Vg%d' % kt, 'Vg'], Vg.ap(kt * 132 + hh * 66, [(1, 65)])) for kt in range(34)]
                    if hh == 0:
                        dst = ('attnTa', attnT.ap(g * CQW + q0, [(1, NQ)], np_=64), None)
                    else:
                        dst = ('attnTb', None, attnT.ap(g * CQW + q0, [(1, NQ)], p0=64, np_=64))
                    jobs.append(dict(NQ=NQ, q=(['QT%d' % qi_ for qi_ in qis], QT.ap(hh * CQW + q0, [(1, NQ)], np_=96)),
                                     ents=ents, mask=None, dst=dst, escale=MLA_SCALE))
            B.attention(jobs, pt, rec, bcs, otmp)
            if stop == 'g0':
                B.P.finalize(nc, st)
                return nc
            B.bar()
        A.release(mS)
        A2.release(mA2)
        xres = A.alloc(16 * 1024, F32)
        HW = 2050
        h2T = A.alloc(8 * HW, BF16)
        op('pool', 'memset', [], ['h2T_c%d' % c for c in range(8)], h2T.ap(), 0.0)
        m1 = A.mark()
        xh = A.alloc(1024, F32)
        h2Th = A.alloc(1024, BF16)
        wo = A2.alloc(8 * 1024, BF16)
        ytmp = A2.alloc(1024, F32)
        dma('pool', wo.ap(0, [(1024, 8), (1, 1024)]), dap('w_out', 0, [[1024, 128], [128 * 1024, 8], [1, 1024]]), [], ['wo'])

        def xres_ap(qi):
            if qi < 16:
                return 'xres%d' % qi, (lambda c0, n, qi=qi: xres.ap(qi * 1024 + c0, [(1, n)]))
            return 'xh', (lambda c0, n: xh.ap(c0, [(1, n)]))

        for qi in range(17):
            xkey, xap = xres_ap(qi)
            dma('sp', xap(0, 1024), dap('xa', qi * 128 * 1024, [[1024, 128], [1, 1024]]), [], [xkey])
            bk = 2 * (qi % 2)
            for n in range(2):
                for c in range(8):
                    op('pe', 'matmul', ['wo'], ['ps%d' % (bk + n)], PS.ap32((bk + n) * 512, [(1, 512)]),
                       lhsT=attnT.ap(c * CQW + qi * 128, [(1, 128)]), rhs=wo.ap(c * 1024 + n * 512, [(1, 512)]),
                       start=(c == 0), stop=(c == 7))
            op('dve', 'tensor_tensor', ['ps%d' % bk, 'ps%d' % (bk + 1), 'gb0_0'], ['ytmp'], out=ytmp.ap(),
               in0=PS.ap32(bk * 512, [(1, 1024)]), in1=B.gb[0][0].ap(), op=ALU.mult)
            op('dve', 'scalar_tensor_tensor', [xkey, 'ytmp'], [xkey], out=xap(0, 1024), in0=xap(0, 1024), scalar=ALPHA,
               in1=ytmp.ap(), op0=ALU.mult, op1=ALU.add)
            B.layernorm(xkey, xap, 'lnb0', ln1g, 'lnb1', ln1b)
            if qi < 16:
                B.make_hT(xkey, xap(0, 1024), 0, 1, lambda c, qi=qi: h2T.ap(c * HW + 1 + qi * 128, [(1, 128)]),
                          lambda c: 'h2T_c%d' % c)
            else:
                B.make_hT(xkey, xap(0, 1024), 0, 1, lambda c: h2Th.ap(c * 128, [(1, 128)]), lambda c: 'h2Th%d' % c)
                op('dve', 'tensor_copy', ['h2Th%d' % c for c in range(8)], ['h2T_c%d' % c for c in range(8)],
                   out=h2T.ap(2049, [(HW, 8), (1, 1)]), in_=h2Th.ap(0, [(128, 8), (1, 1)]))
        B.bar()
        A.release(m1)
        A2.release(A2.lo)
        sets = [dict(h2T=h2T, hkey='h2T', W=HW, ntok=2048, v=0, halo=True, tiles=[xres_ap(qi) for qi in range(16)])]
        B.ffn(sets)
        for qi in range(16):
            xkey, xap = xres_ap(qi)
            dma('sp', dap('xo', qi * 128 * 1024, [[1024, 128], [1, 1024]]), xap(0, 1024), [xkey], ['xo%d' % (qi % 4)])
        B.P.finalize(nc, st)
    return nc


def _l1_inputs(inputs, x0, c0, b, half):
    seq = np.arange(4096) if half == 0 else np.arange(4095, -1, -1)
    cseq = np.arange(256) if half == 0 else np.arange(255, -1, -1)
    xa = np.concatenate([x0[b][seq], c0[b][cseq]], axis=0).reshape(34, 128, 1024)
    cvec = np.stack([inputs['c'][b], inputs['c_ctx']], axis=0)
    cvecT = np.ascontiguousarray(cvec.reshape(2, 8, 128).transpose(2, 1, 0)).reshape(128, 16)
    cw = inputs['l1_conv_w']
    if half == 1:
        cw = cw[::-1]
    conv = np.concatenate([cw, inputs['l1_conv_b'][None]], axis=0)
    convT = np.ascontiguousarray(conv.reshape(4, 22, 128).transpose(2, 1, 0)).reshape(128, 88)
    sel = np.zeros((32, 96), np.float32)
    sel[np.arange(32), 64 + np.arange(32)] = 1.0
    return dict(
        xa=np.ascontiguousarray(xa), cvecT=cvecT, w_ada=inputs['l1_w_ada'], b_ada=inputs['l1_b_ada'],
        w_in=inputs['l1_w_in'], gains=np.concatenate([inputs['l1_cq_gain'], inputs['l1_ckv_gain']]),
        w_uq=inputs['l1_w_uq'], w_ukv=inputs['l1_w_ukv'], w_out=inputs['l1_w_out'],
        ln=np.stack([inputs['l1_ln1_g'], inputs['l1_ln1_b'], inputs['l1_ln2_g'], inputs['l1_ln2_b']]),
        w_up=inputs['l1_w_up'], convT=convT, w_down=inputs['l1_w_down'],
        rope=_rope_table(seq, 32).reshape(32, 128, 32), ident=np.eye(128, dtype=np.float32), sel=sel)


def run_l1(inputs, x0, c0, cores=8, stop=None):
    nc = build_l1(stop)
    in_maps = [_l1_inputs(inputs, x0, c0, c // 2, c % 2) for c in range(cores)]
    res = run_bass_kernel_spmd(nc, in_maps, core_ids=list(range(cores)))
    out = np.zeros((4, 4096, 1024), np.float32)
    for c in range(cores):
        b, half = c // 2, c % 2
        xo = res.results[c]['xo'].reshape(2048, 1024)
        if half == 0:
            out[b, :2048] = xo
        else:
            out[b, 2048:] = xo[::-1]
    return out


def kernel(**inputs):
    inputs = {k: np.ascontiguousarray(np.asarray(v, dtype=np.float32)) for k, v in inputs.items()}
    x0, c0 = run_l0(inputs, cores=8)
    return run_l1(inputs, x0, c0, cores=8)
```
